# Optimizing a Trainium2 kernel written in Bass

```python
import jax, jax.numpy as jnp
from jax import lax
import numpy as np

D_MODEL = 1024
BATCH = 8
SEQ = 4096
DEPTH = 4

N_MIXERS = 2
N_HGRN_LAYERS = (DEPTH + N_MIXERS - 1) // N_MIXERS
N_ATTN_LAYERS = (DEPTH - 1 + N_MIXERS - 1) // N_MIXERS

HGRN_HEAD_DIM = 128
HGRN_HEADS = D_MODEL // HGRN_HEAD_DIM
HGRN_CHUNK = 64

ATTN_HEADS = 16
ATTN_HEAD_DIM = D_MODEL // ATTN_HEADS
DILATED_GROUPS = ((128, 1), (512, 4), (2048, 16))
N_GROUPS = len(DILATED_GROUPS)
ATTN_BLOCK = 128

MLP_HIDDEN = 4 * D_MODEL
NORM_EPS = 1e-6

kernel_name = "hybrid_hgrn2_dilated_attn_trunk"


def _rmsnorm(x, gain):
    xf = x.astype(jnp.float32)
    y = xf * lax.rsqrt(jnp.mean(xf * xf, axis=-1, keepdims=True) + NORM_EPS)
    return (y * gain.astype(jnp.float32)).astype(x.dtype)


def _hgrn2_chunk_scan(q, k, v, log_f):
    b, s, h, dk = q.shape
    dv = v.shape[-1]
    nc = s // HGRN_CHUNK

    def to_chunks(t):
        return t.reshape(b, nc, HGRN_CHUNK, h, t.shape[-1]).transpose(1, 0, 3, 2, 4)

    causal = jnp.tril(jnp.ones((HGRN_CHUNK, HGRN_CHUNK), dtype=bool))

    def step(state, inp):
        qi, ki, vi, gi = inp
        cum = jnp.cumsum(gi, axis=2)
        o_inter = jnp.einsum('bhtk,bhkv->bhtv', qi * jnp.exp(cum), state)
        diff = cum[:, :, :, None, :] - cum[:, :, None, :, :]
        decay = jnp.exp(jnp.where(causal[:, :, None], diff, -jnp.inf))
        scores = jnp.einsum('bhtk,bhtsk,bhsk->bhts', qi, decay, ki)
        o_intra = jnp.einsum('bhts,bhsv->bhtv', scores, vi)
        last = cum[:, :, -1:, :]
        k_dec = ki * jnp.exp(last - cum)
        new_state = (jnp.exp(last[:, :, 0, :])[..., None] * state
                     + jnp.einsum('bhsk,bhsv->bhkv', k_dec, vi))
        return new_state, o_inter + o_intra

    init = jnp.zeros((b, h, dk, dv), jnp.float32)
    _, o = lax.scan(step, init, (to_chunks(q), to_chunks(k), to_chunks(v), to_chunks(log_f)))
    return o.transpose(1, 0, 3, 2, 4).reshape(b, s, h, dv)


def _hgrn2_mixer(h, w_in, lower_bound, out_norm, w_out):
    b, s, _ = h.shape
    proj = h @ w_in
    q, f, i, g = jnp.split(proj, 4, axis=-1)
    ff = f.astype(jnp.float32)
    lb = lower_bound.astype(jnp.float32)
    log_f = jnp.logaddexp(jnp.log(lb), jnp.log1p(-lb) + jax.nn.log_sigmoid(ff))
    k = (1.0 - lb) * jax.nn.sigmoid(-ff)
    qf = jax.nn.silu(q.astype(jnp.float32)) * (HGRN_HEAD_DIM ** -0.5)
    heads = lambda t: t.reshape(b, s, HGRN_HEADS, HGRN_HEAD_DIM)
    o = _hgrn2_chunk_scan(heads(qf), heads(k), heads(i.astype(jnp.float32)), heads(log_f))
    o = o * lax.rsqrt(jnp.mean(o * o, axis=-1, keepdims=True) + NORM_EPS)
    o = o.reshape(b, s, D_MODEL) * out_norm.astype(jnp.float32) * jax.nn.silu(g.astype(jnp.float32))
    return o.astype(h.dtype) @ w_out


def _banded_causal_attention(q, k, v, reach):
    n, l, h, dh = q.shape
    blk = ATTN_BLOCK
    nb = -(-l // blk)
    lp = nb * blk
    qb = jnp.pad(q, ((0, 0), (0, lp - l), (0, 0), (0, 0))).reshape(n, nb, blk, h, dh)
    pad_kv = ((0, 0), (blk, lp - l), (0, 0), (0, 0))
    kp = jnp.pad(k, pad_kv).reshape(n, nb + 1, blk, h, dh)
    vp = jnp.pad(v, pad_kv).reshape(n, nb + 1, blk, h, dh)
    kb = jnp.concatenate([kp[:, :-1], kp[:, 1:]], axis=2)
    vb = jnp.concatenate([vp[:, :-1], vp[:, 1:]], axis=2)
    bidx = jnp.arange(nb)[:, None, None]
    qpos = bidx * blk + jnp.arange(blk)[None, :, None]
    kpos = (bidx - 1) * blk + jnp.arange(2 * blk)[None, None, :]
    dist = qpos - kpos
    mask = (kpos >= 0) & (dist >= 0) & (dist <= reach)
    s = jnp.einsum('nbqhd,nbkhd->nbhqk', qb, kb).astype(jnp.float32) * (dh ** -0.5)
    s = jnp.where(mask[None, :, None], s, -jnp.inf)
    lse = jax.nn.logsumexp(s, axis=-1)
    p = jnp.exp(s - lse[..., None])
    o = jnp.einsum('nbhqk,nbkhd->nbqhd', p.astype(v.dtype), vb).astype(jnp.float32)
    o = o.reshape(n, lp, h, dh)[:, :l]
    lse = lse.transpose(0, 1, 3, 2).reshape(n, lp, h)[:, :l]
    return o, lse


def _dilated_group(q, k, v, dilation, reach):
    b, s, h, dh = q.shape
    l = s // dilation

    def split(t):
        return t.reshape(b, l, dilation, h, dh).transpose(0, 2, 1, 3, 4).reshape(b * dilation, l, h, dh)

    o, lse = _banded_causal_attention(split(q), split(k), split(v), reach)
    o = o.reshape(b, dilation, l, h, dh).transpose(0, 2, 1, 3, 4).reshape(b, s, h, dh)
    lse = lse.reshape(b, dilation, l, h).transpose(0, 2, 1, 3).reshape(b, s, h)
    return o, lse


def _dilated_attention_mixer(h, w_qkv, w_out):
    b, s, _ = h.shape
    proj = (h @ w_qkv).reshape(b, s, N_GROUPS, 3, ATTN_HEADS, ATTN_HEAD_DIM)
    outs, lses = [], []
    for g, (window, dilation) in enumerate(DILATED_GROUPS):
        o, lse = _dilated_group(proj[:, :, g, 0], proj[:, :, g, 1], proj[:, :, g, 2],
                                dilation, window // dilation)
        outs.append(o)
        lses.append(lse)
    weights = jax.nn.softmax(jnp.stack(lses, axis=0), axis=0)
    o = jnp.sum(weights[..., None] * jnp.stack(outs, axis=0), axis=0)
    return o.reshape(b, s, D_MODEL).astype(h.dtype) @ w_out


def _squared_relu_mlp(h, w1, w2):
    a = jax.nn.relu(h @ w1)
    return (a * a) @ w2


def setup_inputs(seed: int = 0) -> dict:
    key = jax.random.key(seed)
    ks = jax.random.split(key, 13)
    nrm = lambda k, shape: jax.random.normal(k, shape, jnp.float32)
    d = D_MODEL
    return {
        "x": nrm(ks[0], (BATCH, SEQ, d)),
        "norm_mix": 1.0 + 0.02 * nrm(ks[1], (DEPTH, d)),
        "norm_mlp": 1.0 + 0.02 * nrm(ks[2], (DEPTH, d)),
        "norm_final": 1.0 + 0.02 * nrm(ks[3], (d,)),
        "hgrn_w_in": nrm(ks[4], (N_HGRN_LAYERS, d, 4 * d)) * d ** -0.5,
        "hgrn_lower_bound": 0.5 * nrm(ks[5], (N_HGRN_LAYERS, d)),
        "hgrn_out_norm": 1.0 + 0.02 * nrm(ks[6], (N_HGRN_LAYERS, d)),
        "hgrn_w_out": nrm(ks[7], (N_HGRN_LAYERS, d, d)) * d ** -0.5,
        "attn_w_qkv": nrm(ks[8], (N_ATTN_LAYERS, d, N_GROUPS * 3 * d)) * d ** -0.5,
        "attn_w_out": nrm(ks[9], (N_ATTN_LAYERS, d, d)) * d ** -0.5,
        "mlp_w1": nrm(ks[10], (DEPTH, d, MLP_HIDDEN)) * d ** -0.5,
        "mlp_w2": nrm(ks[11], (DEPTH, MLP_HIDDEN, d)) * MLP_HIDDEN ** -0.5,
    }


def reference(x, norm_mix, norm_mlp, norm_final, hgrn_w_in, hgrn_lower_bound, hgrn_out_norm,
              hgrn_w_out, attn_w_qkv, attn_w_out, mlp_w1, mlp_w2):
    lb_all = jnp.cumsum(jax.nn.softmax(hgrn_lower_bound.astype(jnp.float32), axis=0), axis=0)
    lb_all = lb_all - lb_all[0]
    for layer in range(DEPTH):
        j = layer // N_MIXERS
        h = _rmsnorm(x, norm_mix[layer])
        if layer % N_MIXERS == 0:
            mix = _hgrn2_mixer(h, hgrn_w_in[j], lb_all[j], hgrn_out_norm[j], hgrn_w_out[j])
        else:
            mix = _dilated_attention_mixer(h, attn_w_qkv[j], attn_w_out[j])
        x = x + mix
        x = x + _squared_relu_mlp(_rmsnorm(x, norm_mlp[layer]), mlp_w1[layer], mlp_w2[layer])
    return _rmsnorm(x, norm_final)
```

```python
import numpy as np
from contextlib import ExitStack
import concourse.bass as bass
import concourse.mybir as mybir
from concourse.bass_utils import run_bass_kernel_spmd

F32 = mybir.dt.float32
BF16 = mybir.dt.bfloat16
AF = mybir.ActivationFunctionType
ALU = mybir.AluOpType

S = 4096
D = 1024
NCH = 8
DEPTH = 4
EPS = 1e-6
SBUF_BASE = 16512
SBUF_END = 229376
SAME_ENGINE_SYNC = True
SEM_EPOCH = 30000

V_NMIX = 0
V_NMLP = 32
V_NFIN = 64
V_LB = 72
V_ONORM = 88
NV = 104


class Buf:
    __slots__ = ("name", "w", "rs", "dsem", "dcnt")

    def __init__(self, name):
        self.name = name
        self.w = None
        self.rs = []
        self.dsem = None
        self.dcnt = 0


class Tl:
    __slots__ = ("t", "b")

    def __init__(self, t, b):
        self.t = t
        self.b = b


class Eng:
    def __init__(self, name):
        self.name = name
        self.ops = []
        self.known = {}
        self.sem = None
        self.cnt = 0


class Sched:
    def __init__(self, nc, stack):
        self.nc = nc
        self.stack = stack
        self.sems = []
        self.engs = {n: Eng(n) for n in ("pe", "act", "dve", "pool", "sp")}
        self.cur = SBUF_BASE
        self.ntl = 0
        self.dma_toks = []

    def tile(self, shape, dtype, name="t"):
        esz = 4 if dtype == F32 else 2
        nbytes = int(np.prod(shape[1:])) * esz
        nbytes = (nbytes + 63) // 64 * 64
        off = self.cur
        self.cur += nbytes
        assert self.cur <= SBUF_END, ("SBUF overflow", name, self.cur - SBUF_BASE)
        self.ntl += 1
        nm = f"{name}_{self.ntl}"
        t = self.nc.alloc_sbuf_tensor_at(nm, list(shape), dtype, offset=off)
        return Tl(t, Buf(nm))

    def mark(self):
        return self.cur

    def reset(self, m):
        self.cur = m

    def new_sem(self):
        h = self.stack.enter_context(self.nc.semaphore(f"sm{len(self.sems)}"))
        self.sems.append(h)
        return len(self.sems) - 1

    def op(self, en, fn, reads=(), writes=(), dma=None):
        e = self.engs[en]
        deps = []
        for b in reads:
            if b.w is not None:
                deps.append((b.w, False))
        for b in writes:
            if b.w is not None:
                deps.append((b.w, False))
            for r in b.rs:
                deps.append((r, True))
        waits = {}
        for (s, v, src, isdma), war in deps:
            if src == en and not isdma:
                if en == "pe" or war or not SAME_ENGINE_SYNC:
                    continue
            if e.known.get(s, 0) < v:
                if waits.get(s, 0) < v:
                    waits[s] = v
        for s, v in waits.items():
            e.known[s] = v
        if dma is not None:
            if dma.dsem is None:
                dma.dsem = self.new_sem()
            dma.dcnt += 16
            tok = (dma.dsem, dma.dcnt, en, True)
            inc = 16
            self.dma_toks.append(tok)
        else:
            if e.sem is None or e.cnt >= SEM_EPOCH:
                e.sem = self.new_sem()
                e.cnt = 0
            e.cnt += 1
            tok = (e.sem, e.cnt, en, False)
            inc = 1
        e.ops.append((list(waits.items()), fn, tok[0], inc))
        for b in writes:
            b.w = tok
            b.rs = []
        for b in reads:
            b.rs.append(tok)
            if len(b.rs) > 64:
                b.rs = self._prune(b.rs)
        return tok

    @staticmethod
    def _prune(rs):
        best = {}
        for t in rs:
            k = (t[0],)
            if k not in best or best[k][1] < t[1]:
                best[k] = t
        return list(best.values())

    def barrier(self):
        toks = []
        for e in self.engs.values():
            if e.sem is not None and e.cnt > 0:
                toks.append((e.sem, e.cnt))
        best = {}
        for t in self.dma_toks:
            if best.get(t[0], 0) < t[1]:
                best[t[0]] = t[1]
        toks.extend(best.items())
        self.dma_toks = [(s, v, "x", True) for s, v in best.items()]
        for e in self.engs.values():
            w = []
            for s, v in toks:
                if s == e.sem and e.name != "sp":
                    pass
                if e.known.get(s, 0) < v:
                    w.append((s, v))
                    e.known[s] = v
            if w:
                e.ops.append((w, None, None, 0))

    def replay(self, en, h):
        e = self.engs[en]
        for waits, fn, sem, inc in e.ops:
            for s, v in waits:
                h.wait_ge(self.sems[s], v)
            if fn is None:
                continue
            ins = fn(h)
            ins.then_inc(self.sems[sem], inc)


def build_program(nc, cfg):
    layers = cfg.get("layers", list(range(DEPTH)))
    do_mixer = cfg.get("mixer", True)
    do_mlp = cfg.get("mlp", True)

    def din(name, shape):
        return nc.dram_tensor(name, list(shape), F32, kind="ExternalInput").ap()

    x_d = din("x", [S, D])
    vec_d = din("vecs", [128, NV])
    ident_d = din("ident", [128, 128])
    amask_d = din("amask", [128, 512])
    bmask_d = din("bmask", [128, 128])
    rmask_d = din("rmask", [128, 2048])
    sel_d = din("sel", [128, 64])
    hwin_d = din("hgrn_w_in", [2, D, 4 * D])
    hwout_d = din("hgrn_w_out", [2, D, D])
    aqkv_d = din("attn_w_qkv", [2, D, 9 * D])
    awout_d = din("attn_w_out", [2, D, D])
    w1_d = din("mlp_w1", [DEPTH, D, 4 * D])
    w2_d = din("mlp_w2", [DEPTH, 4 * D, D])
    out_d = nc.dram_tensor("out", [S, D], F32, kind="ExternalOutput").ap()
    XT_d = nc.dram_tensor("XT", [NCH, 128, S], F32, kind="Internal").ap()
    OB_d = nc.dram_tensor("OB", [NCH, 128, S], BF16, kind="Internal").ap()
    XT_v = XT_d.rearrange("c p t -> p c t")
    OB_v = OB_d.rearrange("c p t -> p c t")
    xt_dram = Buf("XT_dram")
    ob_dram = Buf("OB_dram")

    with ExitStack() as stack:
        K = Sched(nc, stack)
        PS = []
        for i in range(8):
            t = stack.enter_context(nc.psum_tensor(f"ps{i}", [128, 512], F32))
            PS.append(Tl(t, Buf(f"ps{i}")))

        vec = K.tile([128, NV], F32, "vec")
        identf = K.tile([128, 128], F32, "identf")
        identb = K.tile([128, 128], BF16, "identb")
        amask = K.tile([128, 512], BF16, "amask")
        bmask = K.tile([128, 128], F32, "bmask")
        rmask = K.tile([128, 2048], F32, "rmask")
        self_ = K.tile([128, 64], F32, "sel")
        onesD = K.tile([128, 128], BF16, "onesD")
        onesH = K.tile([128, 128], BF16, "onesH")
        lbv = K.tile([128, 16], F32, "lbv")
        omlv = K.tile([128, 16], F32, "omlv")
        nomlv = K.tile([128, 16], F32, "nomlv")
        ctmp = K.tile([128, 16], F32, "ctmp")
        epsc = K.tile([128, 1], F32, "epsc")
        K.op("dve", lambda h: h.memset(epsc.t[:], EPS), writes=[epsc.b])

        K.op("sp", lambda h: h.dma_start(out=vec.t[:], in_=vec_d[:, :]), writes=[vec.b], dma=vec.b)
        K.op("sp", lambda h: h.dma_start(out=identf.t[:], in_=ident_d[:, :]), writes=[identf.b], dma=identf.b)
        K.op("sp", lambda h: h.dma_start(out=bmask.t[:], in_=bmask_d[:, :]), writes=[bmask.b], dma=bmask.b)
        K.op("sp", lambda h: h.dma_start(out=rmask.t[:], in_=rmask_d[:, :]), writes=[rmask.b], dma=rmask.b)
        K.op("sp", lambda h: h.dma_start(out=self_.t[:], in_=sel_d[:, :]), writes=[self_.b], dma=self_.b)
        K.op("pool", lambda h: h.dma_start(out=amask.t[:], in_=amask_d[:, :]), writes=[amask.b], dma=amask.b)
        K.op("pool", lambda h: h.dma_start(out=identb.t[:], in_=ident_d[:, :]), writes=[identb.b], dma=identb.b)
        K.op("dve", lambda h: h.memset(onesD.t[:], 1.0 / 1024.0), writes=[onesD.b])
        K.op("dve", lambda h: h.memset(onesH.t[:], 1.0 / 128.0), writes=[onesH.b])
        K.op("act", lambda h: h.activation(out=ctmp.t[:], in_=vec.t[:, V_LB:V_LB + 16], func=AF.Exp),
             reads=[vec.b], writes=[ctmp.b])
        K.op("dve", lambda h: h.tensor_tensor(out=lbv.t[:, 0:8], in0=ctmp.t[:, 0:8], in1=ctmp.t[:, 8:16], op=ALU.add),
             reads=[ctmp.b], writes=[lbv.b])
        K.op("dve", lambda h: h.reciprocal(out=lbv.t[:, 0:8], in_=lbv.t[:, 0:8]), reads=[lbv.b], writes=[lbv.b])
        K.op("dve", lambda h: h.tensor_tensor(out=lbv.t[:, 8:16], in0=ctmp.t[:, 8:16], in1=lbv.t[:, 0:8], op=ALU.mult),
             reads=[ctmp.b, lbv.b], writes=[lbv.b])
        K.op("dve", lambda h: h.memset(lbv.t[:, 0:8], 0.0), writes=[lbv.b])
        K.op("dve", lambda h: h.tensor_scalar(out=omlv.t[:], in0=lbv.t[:], scalar1=-1.0, scalar2=1.0,
                                              op0=ALU.mult, op1=ALU.add), reads=[lbv.b], writes=[omlv.b])
        K.op("dve", lambda h: h.tensor_scalar(out=nomlv.t[:], in0=lbv.t[:], scalar1=1.0, scalar2=-1.0,
                                              op0=ALU.mult, op1=ALU.add), reads=[lbv.b], writes=[nomlv.b])
        base_mark = K.mark()

        dbg_on = cfg.get("dbg", False)

        def dump(name, tl, n):
            if not dbg_on:
                return
            d = nc.dram_tensor("dbg_" + name, [128, n], F32, kind="ExternalOutput").ap()
            nd = len(tl.t.shape)
            src = tl.t[:, :] if nd == 2 else (tl.t[:, :, :].rearrange("p a t -> p (a t)") if nd == 3
                                              else tl.t[:, :, :, :].rearrange("p a b t -> p (a b t)"))
            K.op("pool", lambda h: h.dma_start(out=d[:, :], in_=src[:, 0:n]), reads=[tl.b], dma=tl.b)
            K.barrier()

        def norm_tile(xt, T, gcol, hout, sq, rstd, psb, hoff=0):
            K.op("act", lambda h: h.activation(out=sq.t[:, :, 0:T], in_=xt.t[:, :, 0:T], func=AF.Square),
                 reads=[xt.b], writes=[sq.b])

            def mm(h):
                ins = None
                for c in range(NCH):
                    ins = h.matmul(psb.t[:, 0:T], onesD.t[:], sq.t[:, c, 0:T], start=(c == 0), stop=(c == NCH - 1))
                return ins
            K.op("pe", mm, reads=[sq.b, onesD.b], writes=[psb.b])
            K.op("act", lambda h: h.activation(out=rstd.t[:, 0:T], in_=psb.t[:, 0:T], func=AF.Sqrt, bias=epsc.t[:, 0:1]),
                 reads=[psb.b, epsc.b], writes=[rstd.b])
            K.op("dve", lambda h: h.reciprocal(out=rstd.t[:, 0:T], in_=rstd.t[:, 0:T]), reads=[rstd.b], writes=[rstd.b])

            def hh(h):
                ins = None
                for c in range(NCH):
                    ins = h.scalar_tensor_tensor(out=hout.t[:, c, hoff:hoff + T], in0=xt.t[:, c, 0:T],
                                                 scalar=vec.t[:, gcol + c:gcol + c + 1], in1=rstd.t[:, 0:T],
                                                 op0=ALU.mult, op1=ALU.mult)
                return ins
            K.op("dve", hh, reads=[xt.b, rstd.b, vec.b], writes=[hout.b])

        def load_w(dst, src_ap, nsplit, eng="pool"):
            KC = dst.t.shape[1]
            v = src_ap.rearrange("(kc p) m -> p kc m", p=128)
            step = KC // nsplit
            for i in range(nsplit):
                a, b = i * step, (i + 1) * step
                K.op(eng, (lambda a, b: lambda h: h.dma_start(out=dst.t[:, a:b, :], in_=v[:, a:b, :]))(a, b),
                     writes=[dst.b], dma=dst.b)

        def phase_in():
            K.reset(base_mark)
            xin = [K.tile([128, D], F32, "xin") for _ in range(2)]
            xo = [K.tile([128, NCH, 512], F32, "xo") for _ in range(2)]
            for g in range(S // 128):
                xi = xin[g % 2]
                K.op("sp", (lambda g, xi: lambda h: h.dma_start(out=xi.t[:], in_=x_d[g * 128:(g + 1) * 128, :]))(g, xi),
                     writes=[xi.b], dma=xi.b)
                xot = xo[(g // 4) % 2]
                for half in range(2):
                    pb = PS[(g * 2 + half) % 4]

                    def tr(h, xi=xi, half=half, pb=pb):
                        ins = None
                        for c4 in range(4):
                            c = half * 4 + c4
                            ins = h.transpose(pb.t[:, c4 * 128:(c4 + 1) * 128], xi.t[:, c * 128:(c + 1) * 128], identf.t[:])
                        return ins
                    K.op("pe", tr, reads=[xi.b, identf.b], writes=[pb.b])
                    en = "act" if half == 0 else "dve"

                    def ev(h, xot=xot, half=half, pb=pb, g=g, en=en):
                        o = xot.t[:, half * 4:(half + 1) * 4, (g % 4) * 128:(g % 4 + 1) * 128]
                        i = pb.t[:, :].rearrange("p (c t) -> p c t", c=4)
                        if en == "act":
                            return h.copy(out=o, in_=i)
                        return h.tensor_copy(out=o, in_=i)
                    K.op(en, ev, reads=[pb.b], writes=[xot.b])
                if g % 4 == 3:
                    t0 = (g // 4) * 512
                    K.op("sp", (lambda t0, xot: lambda h: h.dma_start(out=XT_v[:, :, t0:t0 + 512], in_=xot.t[:]))(t0, xot),
                         reads=[xot.b], writes=[xt_dram], dma=xot.b)
            K.barrier()

        def phase_out():
            K.reset(base_mark)
            T = 512
            xts = [K.tile([128, NCH, T], F32, "xt") for _ in range(2)]
            sq = K.tile([128, NCH, T], BF16, "sq")
            rstd = K.tile([128, T], F32, "rstd")
            yt = K.tile([128, NCH, T], F32, "yt")
            yo = [K.tile([128, D], F32, "yo") for _ in range(2)]
            for i in range(S // T):
                xt = xts[i % 2]
                K.op("sp", (lambda i, xt: lambda h: h.dma_start(out=xt.t[:], in_=XT_v[:, :, i * T:(i + 1) * T]))(i, xt),
                     reads=[xt_dram], writes=[xt.b], dma=xt.b)
                psb = PS[4 + i % 2]
                K.op("act", (lambda xt: lambda h: h.activation(out=sq.t[:], in_=xt.t[:], func=AF.Square))(xt),
                     reads=[xt.b], writes=[sq.b])

                def mm(h, psb=psb):
                    ins = None
                    for c in range(NCH):
                        ins = h.matmul(psb.t[:, 0:T], onesD.t[:], sq.t[:, c, :], start=(c == 0), stop=(c == NCH - 1))
                    return ins
                K.op("pe", mm, reads=[sq.b, onesD.b], writes=[psb.b])
                K.op("act", (lambda psb: lambda h: h.activation(out=rstd.t[:], in_=psb.t[:, 0:T], func=AF.Sqrt, bias=epsc.t[:, 0:1]))(psb),
                     reads=[psb.b, epsc.b], writes=[rstd.b])
                K.op("dve", lambda h: h.reciprocal(out=rstd.t[:], in_=rstd.t[:]), reads=[rstd.b], writes=[rstd.b])

                def yy(h, xt=xt):
                    ins = None
                    for c in range(NCH):
                        ins = h.scalar_tensor_tensor(out=yt.t[:, c, :], in0=xt.t[:, c, :],
                                                     scalar=vec.t[:, V_NFIN + c:V_NFIN + c + 1], in1=rstd.t[:],
                                                     op0=ALU.mult, op1=ALU.mult)
                    return ins
                K.op("dve", yy, reads=[xt.b, rstd.b, vec.b], writes=[yt.b])
                for g4 in range(T // 128):
                    g = i * (T // 128) + g4
                    yot = yo[g % 2]
                    for half in range(2):
                        pb = PS[(g * 2 + half) % 4]

                        def tr(h, half=half, pb=pb, g4=g4):
                            ins = None
                            for c4 in range(4):
                                c = half * 4 + c4
                                ins = h.transpose(pb.t[:, c4 * 128:(c4 + 1) * 128], yt.t[:, c, g4 * 128:(g4 + 1) * 128], identf.t[:])
                            return ins
                        K.op("pe", tr, reads=[yt.b, identf.b], writes=[pb.b])
                        en = "act" if half == 0 else "dve"

                        def ev(h, yot=yot, half=half, pb=pb, en=en):
                            o = yot.t[:, half * 512:(half + 1) * 512]
                            if en == "act":
                                return h.copy(out=o, in_=pb.t[:, :])
                            return h.tensor_copy(out=o, in_=pb.t[:, :])
                        K.op(en, ev, reads=[pb.b], writes=[yot.b])
                    K.op("sp", (lambda g, yot: lambda h: h.dma_start(out=out_d[g * 128:(g + 1) * 128, :], in_=yot.t[:]))(g, yot),
                         reads=[yot.b], dma=yot.b)
            K.barrier()

        def phase_mlp(l):
            K.reset(base_mark)
            T = 256
            NT = S // T
            W1 = K.tile([128, NCH, 4 * D], BF16, "W1")
            W2 = K.tile([128, 32, D], BF16, "W2")
            load_w(W1, w1_d[l], 8)
            load_w(W2, w2_d[l], 8)
            xts = [K.tile([128, NCH, T], F32, "xt") for _ in range(2)]
            hs = [K.tile([128, NCH, T], BF16, "h") for _ in range(2)]
            sq = K.tile([128, NCH, T], BF16, "sq")
            rstd = K.tile([128, T], F32, "rstd")
            a = K.tile([128, 32, T], BF16, "a")
            rr = [K.tile([128, T], F32, "rr") for _ in range(3)]
            gcol = V_NMLP + l * 8

            def load(i):
                xt = xts[i % 2]
                K.op("sp", lambda h: h.dma_start(out=xt.t[:], in_=XT_v[:, :, i * T:(i + 1) * T]),
                     reads=[xt_dram], writes=[xt.b], dma=xt.b)

            load(0)
            norm_tile(xts[0], T, gcol, hs[0], sq, rstd, PS[6])
            for i in range(NT):
                xt, hh = xts[i % 2], hs[i % 2]
                if i + 1 < NT:
                    load(i + 1)
                for m in range(32):
                    pb = PS[m % 3]

                    def mm1(h, m=m, pb=pb, hh=hh):
                        ins = None
                        for kc in range(NCH):
                            ins = h.matmul(pb.t[:, 0:T], W1.t[:, kc, m * 128:(m + 1) * 128], hh.t[:, kc, :],
                                           start=(kc == 0), stop=(kc == NCH - 1))
                        return ins
                    K.op("pe", mm1, reads=[W1.b, hh.b], writes=[pb.b])
                    r = rr[m % 3]
                    K.op("act", (lambda pb, r: lambda h: h.activation(out=r.t[:], in_=pb.t[:, 0:T], func=AF.Relu))(pb, r),
                         reads=[pb.b], writes=[r.b])
                    K.op("pool", (lambda m, r: lambda h: h.tensor_tensor(out=a.t[:, m, :], in0=r.t[:], in1=r.t[:], op=ALU.mult))(m, r),
                         reads=[r.b], writes=[a.b])
                if i + 1 < NT:
                    norm_tile(xts[(i + 1) % 2], T, gcol, hs[(i + 1) % 2], sq, rstd, PS[6])
                for m in range(NCH):
                    pb = PS[3 + m % 2]

                    def mm2(h, m=m, pb=pb):
                        ins = None
                        for kc in range(32):
                            ins = h.matmul(pb.t[:, 0:T], W2.t[:, kc, m * 128:(m + 1) * 128], a.t[:, kc, :],
                                           start=(kc == 0), stop=(kc == 31))
                        return ins
                    K.op("pe", mm2, reads=[W2.b, a.b], writes=[pb.b])
                    K.op("dve", (lambda m, pb, xt: lambda h: h.tensor_tensor(out=xt.t[:, m, :], in0=xt.t[:, m, :], in1=pb.t[:, 0:T],
                                                                           op=ALU.add))(m, pb, xt),
                         reads=[pb.b, xt.b], writes=[xt.b])
                K.op("sp", (lambda i, xt: lambda h: h.dma_start(out=XT_v[:, :, i * T:(i + 1) * T], in_=xt.t[:]))(i, xt),
                     reads=[xt.b], writes=[xt_dram], dma=xt.b)
            K.barrier()


        def phase_hgrn(l):
            j = l // 2
            K.reset(base_mark)
            T = 256
            NT = S // T
            NB = T // 128
            NCK = T // 64
            SCALE = 128.0 ** -0.5
            Win = K.tile([128, NCH, 4 * D], BF16, "Win")
            Wout = K.tile([128, NCH, D], BF16, "Wout")
            load_w(Win, hwin_d[j], 8)
            load_w(Wout, hwout_d[j], 2)
            xts = [K.tile([128, NCH, T], F32, "xt") for _ in range(2)]
            hT = K.tile([128, NCH, T], BF16, "hT")
            sq = K.tile([128, NCH, T], BF16, "sq")
            rstd = K.tile([128, T], F32, "rstd")
            vtok = K.tile([128, NB, D], BF16, "vtok")
            qs = K.tile([128, NCH, T], F32, "qs")
            sg = K.tile([128, NCH, T], BF16, "sg")
            sig = K.tile([128, NCH, T], F32, "sig")
            fv = K.tile([128, NCH, T], F32, "fv")
            kk = K.tile([128, NCH, T], F32, "kk")
            qd = K.tile([128, NCH, T], BF16, "qd")
            kd = K.tile([128, NCH, T], BF16, "kd")
            kdT = K.tile([128, NCH, T], BF16, "kdT")
            kdtok = K.tile([128, NB, NCH, 128], BF16, "kdtok")
            elast = K.tile([128, NCH, NCK], F32, "elast")
            oT = K.tile([128, NCH, T], F32, "oT")
            ob = K.tile([128, NCH, T], BF16, "ob")
            Sf = [K.tile([128, 4, 128], F32, "Sf") for _ in range(2)]
            Sb = [K.tile([128, 4, 128], BF16, "Sb") for _ in range(2)]
            scm = [K.tile([128, 4, 128], BF16, "scm") for _ in range(2)]
            gcol = V_NMIX + l * 8
            lcol = j * 8
            ocol = V_ONORM + j * 8
            gbc = [0]

            def gb():
                gbc[0] += 1
                return PS[6 + gbc[0] % 2]

            for hg in range(2):
                K.op("dve", (lambda hg: lambda h: h.memset(Sf[hg].t[:], 0.0))(hg), writes=[Sf[hg].b])
                K.op("dve", (lambda hg: lambda h: h.memset(Sb[hg].t[:], 0.0))(hg), writes=[Sb[hg].b])

            def load(i):
                xt = xts[i % 2]
                K.op("sp", lambda h: h.dma_start(out=xt.t[:], in_=XT_v[:, :, i * T:(i + 1) * T]),
                     reads=[xt_dram], writes=[xt.b], dma=xt.b)

            load(0)
            for i in range(NT):
                xt = xts[i % 2]
                if i + 1 < NT:
                    load(i + 1)
                norm_tile(xt, T, gcol, hT, sq, rstd, gb())
                if i == 0:
                    dump("hT", hT, NCH * T)
                for blk in range(NB):
                    for half in range(2):
                        pb = gb()

                        def mmv(h, blk=blk, half=half, pb=pb):
                            ins = None
                            for kc in range(NCH):
                                ins = h.matmul(pb.t[:, :], hT.t[:, kc, blk * 128:(blk + 1) * 128],
                                               Win.t[:, kc, 2048 + half * 512:2048 + (half + 1) * 512],
                                               start=(kc == 0), stop=(kc == NCH - 1))
                            return ins
                        K.op("pe", mmv, reads=[hT.b, Win.b], writes=[pb.b])
                        K.op("act", (lambda blk, half, pb: lambda h: h.copy(out=vtok.t[:, blk, half * 512:(half + 1) * 512], in_=pb.t[:, :]))(blk, half, pb),
                             reads=[pb.b], writes=[vtok.b])
                for col0, func, dst in ((0, AF.Silu, qs), (3072, AF.Silu, sg), (1024, AF.Sigmoid, sig)):
                    for hp in range(4):
                        pb = gb()

                        def mmp(h, col0=col0, hp=hp, pb=pb):
                            ins = None
                            for hh in range(2):
                                hd = hp * 2 + hh
                                for kc in range(NCH):
                                    ins = h.matmul(pb.t[:, hh * T:(hh + 1) * T], Win.t[:, kc, col0 + hd * 128:col0 + (hd + 1) * 128],
                                                   hT.t[:, kc, :], start=(kc == 0), stop=(kc == NCH - 1))
                            return ins
                        K.op("pe", mmp, reads=[hT.b, Win.b], writes=[pb.b])
                        K.op("act", (lambda func, dst, hp, pb: lambda h: h.activation(
                            out=dst.t[:, hp * 2:hp * 2 + 2, :], in_=pb.t[:, :].rearrange("p (a t) -> p a t", a=2), func=func))(func, dst, hp, pb),
                            reads=[pb.b], writes=[dst.b])
                if i == 0:
                    dump("vtok", vtok, NB * D)
                    dump("qs", qs, NCH * T)
                    dump("sg", sg, NCH * T)
                    dump("sig", sig, NCH * T)
                def gates1(h):
                    ins = None
                    for hd in range(NCH):
                        ins = h.tensor_scalar(out=fv.t[:, hd, :], in0=sig.t[:, hd, :], scalar1=omlv.t[:, lcol + hd:lcol + hd + 1],
                                              scalar2=lbv.t[:, lcol + hd:lcol + hd + 1], op0=ALU.mult, op1=ALU.add)
                    return ins
                K.op("dve", gates1, reads=[sig.b, omlv.b, lbv.b], writes=[fv.b])

                def gates2(h):
                    ins = None
                    for hd in range(NCH):
                        ins = h.tensor_scalar(out=kk.t[:, hd, :], in0=sig.t[:, hd, :], scalar1=nomlv.t[:, lcol + hd:lcol + hd + 1],
                                              scalar2=omlv.t[:, lcol + hd:lcol + hd + 1], op0=ALU.mult, op1=ALU.add)
                    return ins
                K.op("dve", gates2, reads=[sig.b, omlv.b, nomlv.b], writes=[kk.b])
                fvf = fv.t[:, :, :].rearrange("p a t -> p (a t)")
                cumf = sig.t[:, :, :].rearrange("p a t -> p (a t)")
                qsf = qs.t[:, :, :].rearrange("p a t -> p (a t)")
                kkf = kk.t[:, :, :].rearrange("p a t -> p (a t)")
                K.op("act", lambda h: h.activation(out=fvf, in_=fvf, func=AF.Ln), reads=[fv.b], writes=[fv.b])
                K.op("dve", lambda h: h.tensor_tensor_scan(out=cumf, data0=rmask.t[:, 0:NCH * T], data1=fvf, initial=0.0,
                                                           op0=ALU.mult, op1=ALU.add),
                     reads=[fv.b, rmask.b], writes=[sig.b])
                K.op("act", lambda h: h.activation(out=fvf, in_=cumf, func=AF.Exp), reads=[sig.b], writes=[fv.b])
                K.op("dve", lambda h: h.scalar_tensor_tensor(out=qd.t[:, :, :].rearrange("p a t -> p (a t)"), in0=qsf, scalar=SCALE, in1=fvf,
                                                             op0=ALU.mult, op1=ALU.mult),
                     reads=[qs.b, fv.b], writes=[qd.b])
                K.op("act", lambda h: h.activation(out=qsf, in_=cumf, func=AF.Exp, scale=-1.0), reads=[sig.b], writes=[qs.b])
                K.op("dve", lambda h: h.tensor_tensor(out=kd.t[:, :, :].rearrange("p a t -> p (a t)"), in0=kkf, in1=qsf, op=ALU.mult),
                     reads=[kk.b, qs.b], writes=[kd.b])
                cum3 = cumf.rearrange("p (c s) -> p c s", s=64)
                fv3 = fvf.rearrange("p (c s) -> p c s", s=64)
                K.op("dve", lambda h: h.tensor_tensor(out=fv3, in0=cum3[:, :, 63:64].broadcast_to([128, NCH * NCK, 64]), in1=cum3,
                                                      op=ALU.subtract),
                     reads=[sig.b], writes=[fv.b])
                K.op("act", lambda h: h.activation(out=elast.t[:, :, :].rearrange("p a c -> p (a c)"), in_=cumf[:, 63::64], func=AF.Exp),
                     reads=[sig.b], writes=[elast.b])
                K.op("act", lambda h: h.activation(out=fvf, in_=fvf, func=AF.Exp), reads=[fv.b], writes=[fv.b])
                K.op("pool", lambda h: h.tensor_tensor(out=kdT.t[:, :, :].rearrange("p a t -> p (a t)"), in0=kkf, in1=fvf, op=ALU.mult),
                     reads=[kk.b, fv.b], writes=[kdT.b])
                if i == 0:
                    dump("cum", sig, NCH * T)
                    dump("qd", qd, NCH * T)
                    dump("kd", kd, NCH * T)
                    dump("kdT", kdT, NCH * T)
                    dump("elast", elast, NCH * NCK)
                for blk in range(NB):
                    pb = gb()
                    pbv = pb.t[:, :].bitcast(BF16)

                    def trk(h, blk=blk, pbv=pbv):
                        ins = None
                        for hd in range(NCH):
                            ins = h.transpose(pbv[:, hd * 128:(hd + 1) * 128], kdT.t[:, hd, blk * 128:(blk + 1) * 128], identb.t[:])
                        return ins
                    K.op("pe", trk, reads=[kdT.b, identb.b], writes=[pb.b])
                    K.op("dve", (lambda blk, pbv: lambda h: h.tensor_copy(out=kdtok.t[:, blk, :, :].rearrange("p a k -> p (a k)"), in_=pbv))(blk, pbv),
                         reads=[pb.b], writes=[kdtok.b])
                if i == 0:
                    dump("kdtok", kdtok, NB * NCH * 128)
                for blk in range(NB):
                    c0 = blk * 128
                    for hg in range(2):
                        def mmA(h, hg=hg, c0=c0):
                            ins = None
                            for j4 in range(4):
                                hd = hg * 4 + j4
                                ins = h.matmul(PS[hg].t[:, j4 * 128:(j4 + 1) * 128], kd.t[:, hd, c0:c0 + 128], qd.t[:, hd, c0:c0 + 128],
                                               start=True, stop=True)
                            return ins
                        K.op("pe", mmA, reads=[kd.b, qd.b], writes=[PS[hg].b])
                        K.op("dve", (lambda hg: lambda h: h.tensor_tensor(
                            out=scm[hg].t[:, :, :], in0=PS[hg].t[:, :].rearrange("p (a t) -> p a t", a=4),
                            in1=bmask.t[:, :].unsqueeze(1).broadcast_to([128, 4, 128]), op=ALU.mult))(hg),
                            reads=[PS[hg].b, bmask.b], writes=[scm[hg].b])
                    for hg in range(2):
                        def mmB(h, hg=hg, c0=c0, blk=blk):
                            ins = None
                            for j4 in range(4):
                                hd = hg * 4 + j4
                                h.matmul(PS[2 + hg].t[:, j4 * 128:(j4 + 1) * 128], vtok.t[:, blk, hd * 128:(hd + 1) * 128], scm[hg].t[:, j4, :],
                                         start=(j4 == 0), stop=False)
                                ins = h.matmul(PS[2 + hg].t[:, j4 * 128:j4 * 128 + 64], Sb[hg].t[:, j4, :], qd.t[:, hd, c0:c0 + 64],
                                               start=False, stop=False)
                            return ins
                        K.op("pe", mmB, reads=[vtok.b, scm[hg].b, Sb[hg].b, qd.b], writes=[PS[2 + hg].b])
                    for half in range(2):
                        p0 = half * 64
                        ck = blk * 2 + half
                        for hg in range(2):
                            def mmU(h, hg=hg, blk=blk, p0=p0):
                                ins = None
                                for j4 in range(4):
                                    hd = hg * 4 + j4
                                    ins = h.matmul(PS[4 + hg].t[:, j4 * 128:(j4 + 1) * 128], kdtok.t[p0:p0 + 64, blk, hd, :],
                                                   vtok.t[p0:p0 + 64, blk, hd * 128:(hd + 1) * 128], start=True, stop=True)
                                return ins
                            K.op("pe", mmU, reads=[kdtok.b, vtok.b], writes=[PS[4 + hg].b])

                            def upd(h, hg=hg, ck=ck):
                                ins = None
                                for j4 in range(4):
                                    hd = hg * 4 + j4
                                    ins = h.scalar_tensor_tensor(out=Sf[hg].t[:, j4, :], in0=Sf[hg].t[:, j4, :], scalar=elast.t[:, hd, ck:ck + 1],
                                                                 in1=PS[4 + hg].t[:, j4 * 128:(j4 + 1) * 128], op0=ALU.mult, op1=ALU.add)
                                return ins
                            K.op("dve", upd, reads=[PS[4 + hg].b, elast.b, Sf[hg].b], writes=[Sf[hg].b])
                            K.op("act", (lambda hg: lambda h: h.copy(out=Sb[hg].t[:, :, :], in_=Sf[hg].t[:, :, :]))(hg),
                                 reads=[Sf[hg].b], writes=[Sb[hg].b])
                        if half == 0:
                            for hg in range(2):
                                def mmD(h, hg=hg, c0=c0):
                                    ins = None
                                    for j4 in range(4):
                                        hd = hg * 4 + j4
                                        ins = h.matmul(PS[2 + hg].t[:, j4 * 128 + 64:(j4 + 1) * 128], Sb[hg].t[:, j4, :], qd.t[:, hd, c0 + 64:c0 + 128],
                                                       start=False, stop=True)
                                    return ins
                                K.op("pe", mmD, reads=[Sb[hg].b, qd.b], writes=[PS[2 + hg].b])
                    for hg in range(2):
                        K.op("act", (lambda hg, c0: lambda h: h.copy(out=oT.t[:, hg * 4:(hg + 1) * 4, c0:c0 + 128],
                                                                     in_=PS[2 + hg].t[:, :].rearrange("p (a t) -> p a t", a=4)))(hg, c0),
                             reads=[PS[2 + hg].b], writes=[oT.b])
                if i == 0:
                    dump("oT", oT, NCH * T)
                    dump("Sf0", Sf[0], 512)
                oTf = oT.t[:, :, :].rearrange("p a t -> p (a t)")
                K.op("act", lambda h: h.activation(out=sq.t[:, :, :].rearrange("p a t -> p (a t)"), in_=oTf, func=AF.Square),
                     reads=[oT.b], writes=[sq.b])
                for hp in range(4):
                    pb = gb()

                    def mmn(h, hp=hp, pb=pb):
                        ins = None
                        for hh in range(2):
                            ins = h.matmul(pb.t[:, hh * T:(hh + 1) * T], onesH.t[:], sq.t[:, hp * 2 + hh, :], start=True, stop=True)
                        return ins
                    K.op("pe", mmn, reads=[sq.b, onesH.b], writes=[pb.b])
                    K.op("act", (lambda hp, pb: lambda h: h.activation(out=fv.t[:, hp * 2:hp * 2 + 2, :], in_=pb.t[:, :].rearrange("p (a t) -> p a t", a=2),
                                                                      func=AF.Sqrt, bias=epsc.t[:, 0:1]))(hp, pb),
                         reads=[pb.b, epsc.b], writes=[fv.b])
                K.op("dve", lambda h: h.reciprocal(out=fvf, in_=fvf), reads=[fv.b], writes=[fv.b])
                K.op("dve", lambda h: h.tensor_tensor(out=oTf, in0=oTf, in1=fvf, op=ALU.mult), reads=[oT.b, fv.b], writes=[oT.b])

                def gate(h):
                    ins = None
                    for hd in range(NCH):
                        ins = h.scalar_tensor_tensor(out=ob.t[:, hd, :], in0=oT.t[:, hd, :], scalar=vec.t[:, ocol + hd:ocol + hd + 1],
                                                     in1=sg.t[:, hd, :], op0=ALU.mult, op1=ALU.mult)
                    return ins
                K.op("dve", gate, reads=[oT.b, sg.b, vec.b], writes=[ob.b])
                if i == 0:
                    dump("ob", ob, NCH * T)
                for m in range(NCH):
                    pb = gb()

                    def mmo(h, m=m, pb=pb):
                        ins = None
                        for hd in range(NCH):
                            ins = h.matmul(pb.t[:, 0:T], Wout.t[:, hd, m * 128:(m + 1) * 128], ob.t[:, hd, :],
                                           start=(hd == 0), stop=(hd == NCH - 1))
                        return ins
                    K.op("pe", mmo, reads=[Wout.b, ob.b], writes=[pb.b])
                    K.op("dve", (lambda m, pb, xt: lambda h: h.tensor_tensor(out=xt.t[:, m, :], in0=xt.t[:, m, :], in1=pb.t[:, 0:T],
                                                                           op=ALU.add))(m, pb, xt),
                         reads=[pb.b, xt.b], writes=[xt.b])
                K.op("sp", (lambda i, xt: lambda h: h.dma_start(out=XT_v[:, :, i * T:(i + 1) * T], in_=xt.t[:]))(i, xt),
                     reads=[xt.b], writes=[xt_dram], dma=xt.b)
            K.barrier()

        def phase_attn(l):
            j = l // 2
            K.reset(base_mark)
            gcol = V_NMIX + l * 8
            hT = K.tile([128, NCH, S], BF16, "hTall")
            m1 = K.mark()
            T = 512
            xts = [K.tile([128, NCH, T], F32, "xt") for _ in range(2)]
            sq = K.tile([128, NCH, T], BF16, "sq")
            rstd = K.tile([128, T], F32, "rstd")
            for i in range(S // T):
                xt = xts[i % 2]
                K.op("sp", (lambda i, xt: lambda h: h.dma_start(out=xt.t[:], in_=XT_v[:, :, i * T:(i + 1) * T]))(i, xt),
                     reads=[xt_dram], writes=[xt.b], dma=xt.b)
                norm_tile(xt, T, gcol, hT, sq, rstd, PS[6 + i % 2], hoff=i * T)
            K.barrier()
            K.reset(m1)
            wq = [K.tile([128, 3, NCH, 128], BF16, "wqkv") for _ in range(2)]
            QT = K.tile([64, 2, S], BF16, "QT")
            KT = K.tile([64, 2, S], BF16, "KT")
            VT = K.tile([128, S], BF16, "VT")
            Vp = [K.tile([128, 32, 2, 96], BF16, "Vp") for _ in range(2)]
            acc = K.tile([128, 2, S], F32, "acc")
            Es = [K.tile([128, 2, 256], BF16, "E") for _ in range(4)]
            Pm = [K.tile([128, 2, 256], BF16, "P") for _ in range(4)]
            rec = [K.tile([64, 512], F32, "rec") for _ in range(2)]
            obst = K.tile([64, S], BF16, "obst")
            for sl in range(2):
                K.op("dve", (lambda sl: lambda h: h.memset(Vp[sl].t[:, :, :, :], 0.0))(sl), writes=[Vp[sl].b])
                K.op("dve", (lambda sl: lambda h: h.memset(Vp[sl].t[:, :, :, 64:65], 1.0))(sl), writes=[Vp[sl].b])
            K.op("pool", lambda h: h.memset(acc.t[:, :, :], 0.0), writes=[acc.b])
            wv = aqkv_d[j].rearrange("(kc p) m -> p kc m", p=128)
            DIL = (1, 4, 16)
            its = [(p, g) for p in range(8) for g in range(3)]
            cnt = {"ps": 0, "po": 0, "pp": 0, "e": 0}

            def blk_slice(g, bi):
                d = DIL[g]
                nb = 32 // d
                r, b = bi // nb, bi % nb
                start = b * 128 * d + r
                c0 = r * (S // d) + b * 128
                return slice(start, start + 127 * d + 1, d), slice(c0, c0 + 128), b

            def proj_thunks(n):
                p, g = its[n]
                sl = n % 2
                w = wq[sl]
                d = DIL[g]
                mm_ = 512 // d
                th = []

                def ldw():
                    for wi in range(3):
                        c0 = g * 3072 + wi * 1024 + p * 128
                        for kc in range(NCH):
                            K.op("pool", (lambda wi, c0, kc: lambda h: h.dma_start(out=w.t[:, wi, kc, :], in_=wv[:, kc, c0:c0 + 128]))(wi, c0, kc),
                                 writes=[w.b], dma=w.b)
                th.append(ldw)
                late = []
                for wi, dst in ((2, VT), (0, QT), (1, KT)):
                    for tt in range(8):
                        for hh in ((None,) if wi == 2 else (0, 1)):
                            def pq(wi=wi, dst=dst, tt=tt, hh=hh):
                                cnt["pp"] += 1
                                pb = PS[6 + cnt["pp"] % 2]
                                np_ = 128 if hh is None else 64

                                def mm(h):
                                    ins = None
                                    for kc in range(NCH):
                                        lw = w.t[:, wi, kc, :] if hh is None else w.t[:, wi, kc, hh * 64:(hh + 1) * 64]
                                        ins = h.matmul(pb.t[0:np_, :], lw, hT.t[:, kc, tt * 512:(tt + 1) * 512],
                                                       start=(kc == 0), stop=(kc == NCH - 1))
                                    return ins
                                K.op("pe", mm, reads=[w.b, hT.b], writes=[pb.b])
                                dv = dst.t[:, :] if hh is None else dst.t[:, hh, :]
                                if d == 1:
                                    o_ap = dv[:, tt * 512:(tt + 1) * 512]
                                    i_ap = pb.t[0:np_, :]
                                else:
                                    o_ap = dv.rearrange("p (r m) -> p r m", r=d)[:, :, tt * mm_:(tt + 1) * mm_]
                                    i_ap = pb.t[0:np_, :].rearrange("p (m r) -> p r m", r=d)
                                if cnt["pp"] % 2 == 0:
                                    K.op("dve", lambda h: h.tensor_copy(out=o_ap, in_=i_ap), reads=[pb.b], writes=[dst.b])
                                else:
                                    K.op("act", lambda h: h.copy(out=o_ap, in_=i_ap), reads=[pb.b], writes=[dst.b])
                            (th if wi == 2 else late).append(pq)
                for q4 in range(8):
                    def pv(q4=q4):
                        cnt["pp"] += 1
                        pb = PS[6 + cnt["pp"] % 2]
                        pbv = pb.t[:, :].bitcast(BF16)

                        def mm(h):
                            ins = None
                            for b4 in range(4):
                                _, csl, _ = blk_slice(g, q4 * 4 + b4)
                                ins = h.transpose(pbv[:, b4 * 128:(b4 + 1) * 128], VT.t[:, csl], identb.t[:])
                            return ins
                        K.op("pe", mm, reads=[VT.b, identb.b], writes=[pb.b])
                        K.op("dve", lambda h: h.tensor_copy(out=Vp[sl].t[:, q4 * 4:(q4 + 1) * 4, :, 0:64],
                                                            in_=pbv[:, 0:512].rearrange("p (b h d) -> p b h d", b=4, h=2)),
                             reads=[pb.b], writes=[Vp[sl].b])
                    th.append(pv)
                return th, late

            ublk = {}

            def unit_A(n, bi):
                p, g = its[n]
                sl = n % 2
                tsl, csl, b = blk_slice(g, bi)
                ncol = 128 if b == 0 else 256
                cnt["ps"] += 1
                ps = PS[cnt["ps"] % 4]
                if b > 0:
                    _, cslp, _ = blk_slice(g, bi - 1)

                def mms(h):
                    ins = None
                    for hh in range(2):
                        p0 = hh * 64
                        ins = h.matmul(ps.t[:, hh * 256:hh * 256 + 128], KT.t[:, hh, csl], QT.t[:, hh, csl],
                                       start=True, stop=True)
                        if b > 0:
                            ins = h.matmul(ps.t[:, hh * 256 + 128:hh * 256 + 256], KT.t[:, hh, cslp], QT.t[:, hh, csl],
                                           start=True, stop=True)
                    return ins
                K.op("pe", mms, reads=[KT.b, QT.b], writes=[ps.b])
                ustage = cfg.get("ustage", 9)
                if ustage < 2:
                    return
                cnt["e"] += 1
                E = Es[cnt["e"] % 4]
                P = Pm[cnt["e"] % 4]
                K.op("act", lambda h: h.activation(out=E.t[:, :, 0:ncol], in_=ps.t[:, :].rearrange("p (h c) -> p h c", h=2)[:, :, 0:ncol],
                                                   func=AF.Exp, scale=0.125),
                     reads=[ps.b], writes=[E.b])
                if ustage < 3:
                    return
                K.op("dve", lambda h: h.tensor_tensor(out=P.t[:, :, 0:ncol], in0=E.t[:, :, 0:ncol],
                                                       in1=amask.t[:, :].rearrange("p (h c) -> p h c", h=2)[:, :, 0:ncol], op=ALU.mult),
                     reads=[E.b, amask.b], writes=[P.b])
                ublk[(n, bi)] = P

            def unit_B(n, bi):
                p, g = its[n]
                sl = n % 2
                tsl, csl, b = blk_slice(g, bi)
                P = ublk.pop((n, bi))
                cnt["po"] += 1
                po = PS[4 + cnt["po"] % 2]

                def mmo(h):
                    ins = None
                    for hh in range(2):
                        ins = h.matmul(po.t[0:96, hh * 128:(hh + 1) * 128], Vp[sl].t[:, bi, hh, :], P.t[:, hh, 0:128],
                                       start=True, stop=(b == 0))
                        if b > 0:
                            ins = h.matmul(po.t[0:96, hh * 128:(hh + 1) * 128], Vp[sl].t[:, bi - 1, hh, :], P.t[:, hh, 128:256],
                                           start=False, stop=True)
                    return ins
                K.op("pe", mmo, reads=[Vp[sl].b, P.b], writes=[po.b])
                pov = po.t[0:96, 0:256].rearrange("p (h q) -> p h q", h=2)
                if g == 0:
                    K.op("dve", lambda h: h.tensor_copy(out=acc.t[0:96, :, tsl], in_=pov), reads=[po.b], writes=[acc.b])
                else:
                    K.op("dve", lambda h: h.tensor_tensor(out=acc.t[0:96, :, tsl], in0=acc.t[0:96, :, tsl], in1=pov, op=ALU.add),
                         reads=[po.b, acc.b], writes=[acc.b])

            def finish_pair(p):
                for hh in range(2):
                    for tt in range(8):
                        cnt["pp"] += 1
                        pb = PS[6 + cnt["pp"] % 2]
                        rc = rec[tt % 2]
                        K.op("pe", (lambda hh, tt, pb: lambda h: h.matmul(pb.t[0:64, :], self_.t[:, :], acc.t[:, hh, tt * 512:(tt + 1) * 512],
                                                                         start=True, stop=True))(hh, tt, pb),
                             reads=[self_.b, acc.b], writes=[pb.b])
                        K.op("dve", (lambda pb, rc: lambda h: h.reciprocal(out=rc.t[:, :], in_=pb.t[0:64, :]))(pb, rc),
                             reads=[pb.b], writes=[rc.b])
                        K.op("pool", (lambda hh, tt, rc: lambda h: h.tensor_tensor(out=obst.t[:, tt * 512:(tt + 1) * 512],
                                                                                 in0=acc.t[0:64, hh, tt * 512:(tt + 1) * 512], in1=rc.t[:, :],
                                                                                 op=ALU.mult))(hh, tt, rc),
                             reads=[acc.b, rc.b], writes=[obst.b])
                    K.op("sp", (lambda hh: lambda h: h.dma_start(out=OB_d[p, hh * 64:(hh + 1) * 64, :], in_=obst.t[:, :]))(hh),
                         reads=[obst.b], writes=[ob_dram], dma=obst.b)

            astop = cfg.get("attn_stop", 99)
            e0, l0 = proj_thunks(0)
            for t in e0 + l0:
                t()
            for n in range(len(its) if astop > 4 else (3 if astop > 2 else (1 if astop > 0 else 0))):
                pend, late = proj_thunks(n + 1) if n + 1 < len(its) else ([], [])
                if astop >= 2:
                    LA = 2
                    for bi in range(32 + LA):
                        if bi < 32:
                            unit_A(n, bi)
                        if bi >= LA:
                            unit_B(n, bi - LA)
                        if pend:
                            pend.pop(0)()
                for t in pend + late:
                    t()
                if its[n][1] == 2 and astop >= 4:
                    finish_pair(its[n][0])
            K.barrier()
            if astop < 4:
                return
            K.reset(base_mark)
            T = 512
            Wout = K.tile([128, NCH, D], BF16, "Wout")
            load_w(Wout, awout_d[j], 2)
            xts = [K.tile([128, NCH, T], F32, "xt") for _ in range(2)]
            obs = [K.tile([128, NCH, T], BF16, "obt") for _ in range(2)]
            for i in range(S // T):
                xt, obt = xts[i % 2], obs[i % 2]
                K.op("sp", (lambda i, xt: lambda h: h.dma_start(out=xt.t[:], in_=XT_v[:, :, i * T:(i + 1) * T]))(i, xt),
                     reads=[xt_dram], writes=[xt.b], dma=xt.b)
                K.op("sp", (lambda i, obt: lambda h: h.dma_start(out=obt.t[:], in_=OB_v[:, :, i * T:(i + 1) * T]))(i, obt),
                     reads=[ob_dram], writes=[obt.b], dma=obt.b)
                for m in range(NCH):
                    pb = PS[m % 4]

                    def mmo(h, m=m, pb=pb, obt=obt):
                        ins = None
                        for kc in range(NCH):
                            ins = h.matmul(pb.t[:, 0:T], Wout.t[:, kc, m * 128:(m + 1) * 128], obt.t[:, kc, :],
                                           start=(kc == 0), stop=(kc == NCH - 1))
                        return ins
                    K.op("pe", mmo, reads=[Wout.b, obt.b], writes=[pb.b])
                    K.op("dve", (lambda m, pb, xt: lambda h: h.tensor_tensor(out=xt.t[:, m, :], in0=xt.t[:, m, :], in1=pb.t[:, 0:T],
                                                                           op=ALU.add))(m, pb, xt),
                         reads=[pb.b, xt.b], writes=[xt.b])
                K.op("sp", (lambda i, xt: lambda h: h.dma_start(out=XT_v[:, :, i * T:(i + 1) * T], in_=xt.t[:]))(i, xt),
                     reads=[xt.b], writes=[xt_dram], dma=xt.b)
            K.barrier()

        phase_in()
        for l in layers:
            if do_mixer:
                if l % 2 == 0:
                    phase_hgrn(l)
                else:
                    phase_attn(l)
            if do_mlp:
                phase_mlp(l)
        phase_out()

        with nc.Block() as block:
            @block.sync
            def _(h):
                K.replay("sp", h)

            @block.gpsimd
            def _(h):
                K.replay("pool", h)

            @block.tensor
            def _(h):
                K.replay("pe", h)

            @block.vector
            def _(h):
                K.replay("dve", h)

            @block.scalar
            def _(h):
                K.replay("act", h)
    return nc


def _feat_major(v):
    v = np.asarray(v, np.float32).reshape(-1, NCH, 128)
    return np.ascontiguousarray(v.transpose(2, 0, 1).reshape(128, -1))


def host_consts():
    ident = np.eye(128, dtype=np.float32)
    k = np.arange(128)[:, None]
    q = np.arange(128)[None, :]
    amask = np.concatenate([(k <= q), (k >= q), (k <= q), (k >= q)], axis=1).astype(np.float32)
    bmask = ((k <= q) & ((k // 64) == (q // 64))).astype(np.float32)
    rmask = np.ones((128, 2048), np.float32)
    rmask[:, ::64] = 0.0
    sel = np.zeros((128, 64), np.float32)
    sel[64, :] = 1.0
    return dict(ident=ident, amask=amask, bmask=bmask, rmask=rmask, sel=sel)


def make_in_maps(inputs, n_cores=8):
    vecs = np.concatenate([
        _feat_major(inputs["norm_mix"]), _feat_major(inputs["norm_mlp"]),
        _feat_major(np.asarray(inputs["norm_final"]).reshape(1, -1)),
        _feat_major(inputs["hgrn_lower_bound"]), _feat_major(inputs["hgrn_out_norm"])], axis=1)
    assert vecs.shape == (128, NV)
    consts = host_consts()
    shared = dict(vecs=np.ascontiguousarray(vecs), **consts)
    for k in ("hgrn_w_in", "hgrn_w_out", "attn_w_qkv", "attn_w_out", "mlp_w1", "mlp_w2"):
        shared[k] = np.ascontiguousarray(np.asarray(inputs[k], np.float32))
    x = np.asarray(inputs["x"], np.float32)
    return [dict(x=np.ascontiguousarray(x[b]), **shared) for b in range(n_cores)]


def kernel(**inputs):
    nc = bass.Bass("TRN2", target_bir_lowering=False)
    build_program(nc, {})
    in_maps = make_in_maps(inputs)
    res = run_bass_kernel_spmd(nc, in_maps, core_ids=list(range(8)))
    return np.stack([np.asarray(r["out"], np.float32) for r in res.results], axis=0)
```
